# Optimizing a Trainium2 kernel written in Bass

```python
import jax, jax.numpy as jnp
from jax import lax
import numpy as np

D_MODEL = 4096
BATCH = 2
SEQ = 8192
DEPTH = 1

CHUNK = 64
D_FF = ((8 * D_MODEL // 3 + 127) // 128) * 128
MACARON_WEIGHT = 0.5
NORM_EPS = 1e-6

D_HGRN = D_MODEL // 2
HGRN_EXPAND = 128
HGRN_HEADS = D_HGRN // HGRN_EXPAND
HGRN_DK = HGRN_EXPAND
HGRN_DV = D_HGRN // HGRN_HEADS

D_ATTN = D_MODEL // 2
ATTN_HEADS = 16
ATTN_HEAD_DIM = D_ATTN // ATTN_HEADS
LEFT_CHUNKS = 8
BAND = (LEFT_CHUNKS + 1) * CHUNK
MAX_REL = 256

IN_SIZES = (D_HGRN, D_HGRN, D_HGRN, D_HGRN, D_ATTN, D_ATTN, D_ATTN, D_MODEL, D_MODEL)
IN_COLS = int(sum(IN_SIZES))
IN_SPLITS = tuple(int(s) for s in np.cumsum(IN_SIZES)[:-1])

kernel_name = "hybrid_hgrn2_chunkattn_macaron_sandwich"


def rmsnorm(x, g):
    xf = x.astype(jnp.float32)
    y = xf * lax.rsqrt(jnp.mean(xf * xf, axis=-1, keepdims=True) + NORM_EPS)
    return (y * g.astype(jnp.float32)).astype(x.dtype)


def swiglu_half_step(x, pre_g, post_g, w1, w3, w2):
    h = rmsnorm(x, pre_g)
    y = (jax.nn.silu(h @ w1) * (h @ w3)) @ w2
    return x + MACARON_WEIGHT * rmsnorm(y, post_g)


def hgrn2_mixer(q_raw, f_raw, i_raw, g_raw, lb, norm_g):
    B, S, _ = q_raw.shape
    nc = S // CHUNK
    dtype = q_raw.dtype

    def to_chunks(t, d):
        return t.reshape(B, nc, CHUNK, HGRN_HEADS, d).transpose(1, 0, 3, 2, 4).astype(jnp.float32)

    f = lb + (1.0 - lb) * jax.nn.sigmoid(f_raw.astype(jnp.float32))
    q = to_chunks(jax.nn.silu(q_raw.astype(jnp.float32)), HGRN_DK)
    k = to_chunks(1.0 - f, HGRN_DK)
    logf = to_chunks(jnp.log(f), HGRN_DK)
    v = to_chunks(i_raw, HGRN_DV)
    tri = jnp.tril(jnp.ones((CHUNK, CHUNK), dtype=bool))[:, :, None]

    def step(state, xs):
        qc, kc, vc, lfc = xs
        a = jnp.cumsum(lfc, axis=2)
        diff = a[:, :, :, None, :] - a[:, :, None, :, :]
        decay = jnp.exp(jnp.where(tri, diff, -jnp.inf))
        scores = jnp.einsum('bhtsk,bhsk->bhts', decay * qc[:, :, :, None, :], kc)
        o = jnp.einsum('bhts,bhsv->bhtv', scores, vc)
        o = o + jnp.einsum('bhtk,bhkv->bhtv', qc * jnp.exp(a), state)
        a_last = a[:, :, -1:, :]
        new_state = jnp.exp(a_last[:, :, 0, :])[..., None] * state + jnp.einsum(
            'bhsk,bhsv->bhkv', kc * jnp.exp(a_last - a), vc)
        return new_state, o

    s0 = jnp.zeros((B, HGRN_HEADS, HGRN_DK, HGRN_DV), jnp.float32)
    _, o = lax.scan(step, s0, (q, k, v, logf))
    o = o.transpose(1, 0, 3, 2, 4).reshape(B, S, HGRN_HEADS, HGRN_DV)
    o = o * lax.rsqrt(jnp.mean(o * o, axis=-1, keepdims=True) + NORM_EPS)
    o = o * norm_g.astype(jnp.float32).reshape(HGRN_HEADS, HGRN_DV)
    o = o.reshape(B, S, D_HGRN) * jax.nn.silu(g_raw.astype(jnp.float32))
    return o.astype(dtype)


def chunk_band_attention(q_raw, k_raw, v_raw, rel_bias):
    B, S, _ = q_raw.shape
    nc = S // CHUNK
    pad = LEFT_CHUNKS * CHUNK

    def heads(t):
        return t.reshape(B, S, ATTN_HEADS, ATTN_HEAD_DIM).transpose(0, 2, 1, 3)

    q = heads(q_raw) * (ATTN_HEAD_DIM ** -0.5)
    k = jnp.pad(heads(k_raw), ((0, 0), (0, 0), (pad, 0), (0, 0)))
    v = jnp.pad(heads(v_raw), ((0, 0), (0, 0), (pad, 0), (0, 0)))
    q_chunks = q.reshape(B, ATTN_HEADS, nc, CHUNK, ATTN_HEAD_DIM).transpose(2, 0, 1, 3, 4)

    qi = jnp.arange(CHUNK)[:, None]
    kj = jnp.arange(BAND)[None, :]
    rel = qi - kj + pad
    bias = rel_bias.astype(jnp.float32)[:, jnp.clip(rel, -MAX_REL, MAX_REL) + MAX_REL]

    def one_chunk(args):
        c, q_blk = args
        start = c * CHUNK
        k_band = lax.dynamic_slice_in_dim(k, start, BAND, axis=2)
        v_band = lax.dynamic_slice_in_dim(v, start, BAND, axis=2)
        s = jnp.einsum('bhqd,bhkd->bhqk', q_blk, k_band).astype(jnp.float32) + bias
        s = jnp.where(kj >= pad - start, s, -jnp.inf)
        p = jax.nn.softmax(s, axis=-1).astype(v_band.dtype)
        return jnp.einsum('bhqk,bhkd->bhqd', p, v_band)

    out = lax.map(one_chunk, (jnp.arange(nc, dtype=jnp.int32), q_chunks))
    return out.transpose(1, 0, 3, 2, 4).reshape(B, S, D_ATTN)


def setup_inputs(seed: int = 0) -> dict:
    key = jax.random.key(seed)
    ks = jax.random.split(key, 20)
    f32 = jnp.float32

    def w(k, shape, fan_in):
        return jax.random.normal(k, shape, f32) * (fan_in ** -0.5)

    def gain(k, shape):
        return 1.0 + 0.1 * jax.random.normal(k, shape, f32)

    return {
        "x": jax.random.normal(ks[0], (BATCH, SEQ, D_MODEL), f32),
        "ffn1_pre_g": gain(ks[1], (DEPTH, D_MODEL)),
        "ffn1_post_g": gain(ks[2], (DEPTH, D_MODEL)),
        "ffn1_w1": w(ks[3], (DEPTH, D_MODEL, D_FF), D_MODEL),
        "ffn1_w3": w(ks[4], (DEPTH, D_MODEL, D_FF), D_MODEL),
        "ffn1_w2": w(ks[5], (DEPTH, D_FF, D_MODEL), D_FF),
        "mix_pre_g": gain(ks[6], (DEPTH, D_MODEL)),
        "mix_post_g": gain(ks[7], (DEPTH, D_MODEL)),
        "w_in": w(ks[8], (DEPTH, D_MODEL, IN_COLS), D_MODEL),
        "b_gate": 0.1 * jax.random.normal(ks[9], (DEPTH, 2, D_MODEL), f32),
        "hgrn_lb_logits": 0.5 * jax.random.normal(ks[10], (DEPTH + 1, D_HGRN), f32),
        "hgrn_norm_g": gain(ks[11], (DEPTH, D_HGRN)),
        "rel_bias": 0.2 * jax.random.normal(ks[12], (DEPTH, ATTN_HEADS, 2 * MAX_REL + 1), f32),
        "w_up_a": w(ks[13], (DEPTH, D_HGRN, D_MODEL), D_HGRN),
        "w_up_b": w(ks[14], (DEPTH, D_ATTN, D_MODEL), D_ATTN),
        "w_out": w(ks[15], (DEPTH, D_MODEL, D_MODEL), D_MODEL),
        "ffn2_pre_g": gain(ks[16], (DEPTH, D_MODEL)),
        "ffn2_post_g": gain(ks[17], (DEPTH, D_MODEL)),
        "ffn2_w1": w(ks[18], (DEPTH, D_MODEL, D_FF), D_MODEL),
        "ffn2_w3": w(ks[19], (DEPTH, D_MODEL, D_FF), D_MODEL),
        "ffn2_w2": w(jax.random.fold_in(key, 99), (DEPTH, D_FF, D_MODEL), D_FF),
    }


def reference(x, ffn1_pre_g, ffn1_post_g, ffn1_w1, ffn1_w3, ffn1_w2,
              mix_pre_g, mix_post_g, w_in, b_gate, hgrn_lb_logits, hgrn_norm_g,
              rel_bias, w_up_a, w_up_b, w_out,
              ffn2_pre_g, ffn2_post_g, ffn2_w1, ffn2_w3, ffn2_w2):
    lower_bounds = jnp.cumsum(jax.nn.softmax(hgrn_lb_logits.astype(jnp.float32), axis=0), axis=0)
    for layer in range(DEPTH):
        x = swiglu_half_step(x, ffn1_pre_g[layer], ffn1_post_g[layer],
                             ffn1_w1[layer], ffn1_w3[layer], ffn1_w2[layer])
        h = rmsnorm(x, mix_pre_g[layer])
        proj = h @ w_in[layer]
        qa, fa, ia, ga, qb, kb, vb, gate_a, gate_b = jnp.split(proj, IN_SPLITS, axis=-1)
        y_a = hgrn2_mixer(qa, fa, ia, ga, lower_bounds[layer], hgrn_norm_g[layer]) @ w_up_a[layer]
        y_b = chunk_band_attention(qb, kb, vb, rel_bias[layer]) @ w_up_b[layer]
        g_a = jax.nn.sigmoid(gate_a + b_gate[layer, 0])
        g_b = jax.nn.sigmoid(gate_b + b_gate[layer, 1])
        y = (g_a * y_a + g_b * y_b) @ w_out[layer]
        x = x + rmsnorm(y, mix_post_g[layer])
        x = swiglu_half_step(x, ffn2_pre_g[layer], ffn2_post_g[layer],
                             ffn2_w1[layer], ffn2_w3[layer], ffn2_w2[layer])
    return x
```

```python
import numpy as np
import concourse.bass as bass
import concourse.mybir as mybir
from contextlib import ExitStack

F32 = mybir.dt.float32
BF16 = mybir.dt.bfloat16
I32 = mybir.dt.int32
AF = mybir.ActivationFunctionType
ALU = mybir.AluOpType
AX = mybir.AxisListType

ENGS = ("pe", "act", "dve", "pool", "sp")


class Op:
    __slots__ = ("eng", "fn", "deps", "signal", "count", "lane", "waits", "group")


class Prog:
    def __init__(self):
        self.ops = []
        self.lw = {}
        self.rd = {}
        self.lane_tot = {}
        self.lane_last = {}
        self.planning = False

    def add(self, eng, fn, reads=(), writes=(), lane=None, extra=(), group=None):
        if self.planning:
            return None
        op = Op()
        op.group = group
        op.eng = eng
        op.fn = fn
        op.lane = lane
        op.signal = False
        op.count = 0
        deps = set(extra)
        for t in reads:
            w = self.lw.get(t)
            if w is not None:
                deps.add(w)
        for t in writes:
            w = self.lw.get(t)
            if w is not None:
                if group is not None and w.group == group:
                    deps.update(w.deps)
                else:
                    deps.add(w)
            r = self.rd.get(t)
            if r:
                deps.update(r.values())
        deps.discard(op)
        flat = set()
        for d in deps:
            if d.eng is None:
                flat.update(d.deps)
            else:
                flat.add(d)
        deps = flat
        op.deps = deps
        if eng is None:
            for t in writes:
                self.lw[t] = op
                self.rd[t] = {}
            return op
        for d in deps:
            if d.lane is None:
                if d.eng == "pe" and eng == "pe" and lane is None:
                    continue
                d.signal = True
        for t in reads:
            self.rd.setdefault(t, {})[(eng, lane)] = op
        for t in writes:
            self.lw[t] = op
            self.rd[t] = {}
        if lane is not None:
            self.lane_tot[lane] = self.lane_tot.get(lane, 0) + 16
            op.count = self.lane_tot[lane]
            self.lane_last[lane] = op
        self.ops.append(op)
        return op

    def final_wait(self, eng="sp"):
        ex = [op for l, op in self.lane_last.items()
              if (isinstance(l, tuple) and str(l[0]).startswith("st")) or (isinstance(l, str) and l.startswith("st"))]
        return self.add(eng, None, extra=ex)

    def claim(self, *tokens):
        return self.add(None, None, writes=list(tokens))

    def finalize(self):
        cnt = {e: 0 for e in ENGS}
        for op in self.ops:
            if op.lane is None and op.signal:
                cnt[op.eng] += 1
                op.count = cnt[op.eng]
        waited = {e: {} for e in ENGS}
        for op in self.ops:
            need = {}
            for d in op.deps:
                if d.lane is not None:
                    key = ("lane", d.lane)
                else:
                    if d.eng == "pe" and op.eng == "pe" and op.lane is None:
                        continue
                    key = ("eng", d.eng)
                need[key] = max(need.get(key, 0), d.count)
            w = waited[op.eng]
            op.waits = [(k, v) for k, v in need.items() if w.get(k, 0) < v]
            for k, v in op.waits:
                w[k] = v

    def emit(self, nc, st):
        self.finalize()
        esem = {e: st.enter_context(nc.semaphore("e_" + e)) for e in ENGS}
        lsem = {l: st.enter_context(nc.semaphore("l_%s" % (l,))) for l in self.lane_tot}
        per = {e: [op for op in self.ops if op.eng == e] for e in ENGS}
        block = st.enter_context(nc.Block())

        def run(engh, ops):
            for op in ops:
                for (k, v) in op.waits:
                    s = esem[k[1]] if k[0] == "eng" else lsem[k[1]]
                    engh.wait_ge(s, v)
                if op.fn is None:
                    continue
                ins = op.fn(engh)
                if op.lane is not None:
                    ins.then_inc(lsem[op.lane], 16)
                elif op.signal:
                    ins.then_inc(esem[op.eng], 1)

        @block.tensor
        def _(e):
            run(e, per["pe"])

        @block.scalar
        def _(e):
            run(e, per["act"])

        @block.vector
        def _(e):
            run(e, per["dve"])

        @block.gpsimd
        def _(e):
            run(e, per["pool"])

        @block.sync
        def _(e):
            run(e, per["sp"])


class Ctx:
    def __init__(self, nc, st, arena_kib=200):
        self.nc = nc
        self.st = st
        self.P = Prog()
        self.arena = st.enter_context(nc.sbuf_tensor("arena", [128, arena_kib * 256], F32))
        self.banks = [st.enter_context(nc.psum_tensor("bank%d" % i, [128, 512], F32)) for i in range(8)]
        self.uid = 0

    def view(self, off_bytes, nelem, dtype, pat=None, **kw):
        sz = 2 if dtype == BF16 else 4
        assert off_bytes % 4 == 0 and (nelem * sz) % 4 == 0
        a = self.arena[:, off_bytes // 4:(off_bytes + nelem * sz) // 4]
        if dtype != F32:
            a = a.bitcast(dtype)
        if pat is not None:
            a = a.rearrange(pat, **kw)
        return a

    def bank(self, i, dtype=F32):
        b = self.banks[i][:, :]
        if dtype != F32:
            b = b.bitcast(dtype)
        return b


KIB = 1024


class WRing:
    def __init__(self, cx, base, slot_bytes, nslots):
        self.cx = cx
        self.base = base
        self.slot_bytes = slot_bytes
        self.n = nslots
        self.reqs = []
        self.issued = 0
        self.used = 0
        self.done_upto = -1

    def slot_off(self, i):
        return self.base + (i % self.n) * self.slot_bytes

    def request(self, parts):
        if self.cx.P.planning:
            self.reqs.append(parts)
            return len(self.reqs) - 1
        i = self.used
        self.used += 1
        assert len(self.reqs[i]) == len(parts)
        assert i <= self.done_upto + self.n, "ring over-subscribed: call done() on earlier requests first"
        self.issue_upto(self.done_upto + self.n)
        return i

    def done(self, i):
        if self.cx.P.planning:
            return
        self.done_upto = max(self.done_upto, i)
        self.issue_upto(self.done_upto + self.n)

    def issue_upto(self, idx):
        cx = self.cx
        idx = min(idx, len(self.reqs) - 1)
        while self.issued <= idx:
            i = self.issued
            so = self.slot_off(i)
            tok = ("wslot", i % self.n)
            for (boff, nelem, pat, kw, src) in self.reqs[i]:
                dst = cx.view(so + boff, nelem, BF16, pat, **kw)
                cx.P.add("pool", (lambda e, dst=dst, src=src: e.dma_start(out=dst, in_=src)),
                         writes=[tok], lane=("w", i % self.n), group=("wreq", i))
            self.issued += 1

    def views(self, i):
        so = self.slot_off(i)
        return [self.cx.view(so + boff, nelem, BF16, pat, **kw) for (boff, nelem, pat, kw, src) in self.reqs[i]]

    def tok(self, i):
        return ("wslot", i % self.n)


D = 4096
KC = D // 128
T = 512
OFF_H = 0
OFF_U = 32 * KIB
OFF_W = 118 * KIB
WSLOT = 16 * KIB
NSLOT = 3
OFF_S = 166 * KIB
S_TMP = OFF_S
S_ID = OFF_S + 4 * KIB
S_STAT = S_ID + 256
S_GFM = S_STAT + 128
S_END = S_GFM + 3 * 128
EPS = 1e-6


def emit_consts(cx, ident_d, gains_fm_d):
    P = cx.P
    idv = cx.view(S_ID, 128, BF16)
    P.add("pool", lambda e: e.dma_start(out=idv, in_=ident_d), writes=["ident"], lane="const")
    gv = cx.view(S_GFM, 3 * KC, F32)
    P.add("sp", lambda e: e.dma_start(out=gv, in_=gains_fm_d), writes=["gfm"], lane="const2")


def prologue(cx, src, src_name, r0, which_gain):
    P = cx.P
    hT = cx.view(OFF_H, KC * T, BF16, "p (a b) -> p a b", b=T)
    ident = cx.view(S_ID, 128, BF16)
    gfm = cx.view(S_GFM + which_gain * 128, KC, F32)
    stat = cx.view(S_STAT, 32, F32)
    hb = cx.view(OFF_U + 32 * KIB, D, BF16)
    junk = cx.view(OFF_U + 40 * KIB, D, BF16)
    P.claim("RU")
    P.claim("RH")
    for s in range(T // 128):
        xs = cx.view(OFF_U + (s % 2) * 16 * KIB, D, F32)
        xtok = ("xs", s % 2)
        P.add("sp", lambda e, xs=xs, s=s: e.dma_start(out=xs, in_=src[r0 + 128 * s:r0 + 128 * (s + 1), :]),
              reads=["RU", ("dram", src_name, r0 // T)], writes=[xtok], lane=("xs", s % 2))
        ssq = stat[:, s:s + 1]
        rs = stat[:, 8 + s:9 + s]
        P.add("act", lambda e, xs=xs, ssq=ssq: e.activation(out=junk, in_=xs, func=AF.Square, accum_out=ssq),
              reads=[xtok, "RU"], writes=["junk", ("ssq", s)])
        P.add("act", lambda e, ssq=ssq, rs=rs: e.activation(out=rs, in_=ssq, func=AF.Sqrt, scale=1.0 / D, bias=EPS),
              reads=[("ssq", s)], writes=[("rs", s)])
        P.add("dve", lambda e, rs=rs: e.reciprocal(out=rs, in_=rs), reads=[("rs", s)], writes=[("rs", s)])
        P.add("dve", lambda e, xs=xs, rs=rs: e.tensor_scalar(out=hb, in0=xs, scalar1=rs, scalar2=None, op0=ALU.mult),
              reads=[xtok, ("rs", s), "RU"], writes=["hb"])
        for g in range(KC // 8):
            bi = 4 + (g % 2)
            bk = cx.bank(bi, BF16)
            btok = ("bank", bi)
            for j in range(8):
                kc = g * 8 + j
                P.add("pe", lambda e, bk=bk, j=j, kc=kc: e.transpose(out=bk[:, j * 128:(j + 1) * 128],
                                                                   in_=hb[:, kc * 128:(kc + 1) * 128], identity=ident),
                      reads=["hb", "ident", "RU"], writes=[btok])
            for j in range(8):
                kc = g * 8 + j
                dst = hT[:, kc, s * 128:(s + 1) * 128]
                sc = gfm[:, kc:kc + 1]
                if g % 2 == 0:
                    P.add("act", lambda e, dst=dst, bk=bk, j=j, sc=sc: e.activation(
                        out=dst, in_=bk[:, j * 128:(j + 1) * 128], func=AF.Copy, scale=sc),
                        reads=["gfm", btok, "RH"], writes=[("hT", kc, s)])
                else:
                    P.add("dve", lambda e, dst=dst, bk=bk, j=j, sc=sc: e.tensor_scalar(
                        out=dst, in0=bk[:, j * 128:(j + 1) * 128], scalar1=sc, scalar2=None, op0=ALU.mult),
                        reads=["gfm", btok, "RH"], writes=[("hT", kc, s)])


def hT_reads(kc):
    return [("hT", kc, s) for s in range(4)] + ["RH"]


def ffn_core(cx, ring, w1, w3, w2, NFC):
    P = cx.P
    hT = cx.view(OFF_H, KC * T, BF16, "p (a b) -> p a b", b=T)
    u = cx.view(OFF_U, NFC * T, BF16, "p (a b) -> p a b", b=T)
    y = cx.view(OFF_H, 4 * D, BF16, "p (a b) -> p a b", b=D)
    w1r = w1.rearrange("(kc p) f -> p kc f", p=128)
    w3r = w3.rearrange("(kc p) f -> p kc f", p=128)
    w2r = w2.rearrange("(c p) d -> p c d", p=128)
    P.claim("RU")
    for c in range(NFC):
        parts = [(0, KC * 128, "p (a b) -> p a b", dict(b=128), w1r[:, :, c * 128:(c + 1) * 128]),
                 (8 * KIB, KC * 128, "p (a b) -> p a b", dict(b=128), w3r[:, :, c * 128:(c + 1) * 128])]
        ri = ring.request(parts)
        if P.planning:
            continue
        wa, wb = ring.views(ri)
        wtok = ring.tok(ri)
        ba, bb = c % 2, 2 + c % 2
        pa, pb = cx.bank(ba), cx.bank(bb)
        for kc in range(KC):
            P.add("pe", lambda e, pa=pa, wa=wa, kc=kc: e.matmul(pa, lhsT=wa[:, kc, :], rhs=hT[:, kc, :],
                                                               start=(kc == 0), stop=(kc == KC - 1)),
                  reads=[wtok] + hT_reads(kc), writes=[("bank", ba)])
        for kc in range(KC):
            P.add("pe", lambda e, pb=pb, wb=wb, kc=kc: e.matmul(pb, lhsT=wb[:, kc, :], rhs=hT[:, kc, :],
                                                               start=(kc == 0), stop=(kc == KC - 1)),
                  reads=[wtok] + hT_reads(kc), writes=[("bank", bb)])
        ring.done(ri)
        stmp = cx.view(S_TMP + (c % 2) * 2 * KIB, T, F32)
        P.add("act", lambda e, stmp=stmp, pa=pa: e.activation(out=stmp, in_=pa, func=AF.Silu),
              reads=[("bank", ba)], writes=[("stmp", c % 2)])
        P.add("dve", lambda e, stmp=stmp, pb=pb, c=c: e.tensor_tensor(out=u[:, c, :], in0=stmp, in1=pb, op=ALU.mult),
              reads=[("stmp", c % 2), ("bank", bb), "RU"], writes=[("u", c)])
    G = 8
    P.claim("RH")
    for r in range(4):
        for c0 in range(0, NFC, G):
            ng = min(G, NFC - c0)
            parts = [(0, ng * 1024, "p (a b) -> p a b", dict(b=1024), w2r[:, c0:c0 + ng, r * 1024:(r + 1) * 1024])]
            ri = ring.request(parts)
            if P.planning:
                continue
            (wv,) = ring.views(ri)
            wtok = ring.tok(ri)
            for cl in range(ng):
                c = c0 + cl
                for s in range(4):
                    for hf in range(2):
                        bi = s * 2 + hf
                        P.add("pe", lambda e, bi=bi, c=c, s=s, cl=cl, hf=hf, wv=wv: e.matmul(
                            cx.bank(bi), lhsT=u[:, c, s * 128:(s + 1) * 128], rhs=wv[:, cl, hf * 512:(hf + 1) * 512],
                            start=(c == 0), stop=(c == NFC - 1)),
                            reads=[wtok, ("u", c), "RU"], writes=[("bank", bi)])
            ring.done(ri)
        if P.planning:
            continue
        for s in range(4):
            for hf in range(2):
                bi = s * 2 + hf
                dst = y[:, s, r * 1024 + hf * 512: r * 1024 + (hf + 1) * 512]
                if bi % 2 == 0:
                    P.add("act", lambda e, dst=dst, bi=bi: e.activation(out=dst, in_=cx.bank(bi), func=AF.Copy),
                          reads=[("bank", bi), "RH"], writes=[("y", s, r, hf)])
                else:
                    P.add("dve", lambda e, dst=dst, bi=bi: e.tensor_copy(out=dst, in_=cx.bank(bi)),
                          reads=[("bank", bi), "RH"], writes=[("y", s, r, hf)])


def y_reads(s):
    return [("y", s, r, hf) for r in range(4) for hf in range(2)] + ["RH"]


def epilogue(cx, src, src_name, dst, dst_name, r0, gain_row_d, res_scale):
    P = cx.P
    y = cx.view(OFF_H, 4 * D, BF16, "p (a b) -> p a b", b=D)
    stat = cx.view(S_STAT, 32, F32)
    gbc = cx.view(OFF_U + 32 * KIB, D, F32)
    tt = cx.view(OFF_U + 48 * KIB, D, F32)
    junk = cx.view(OFF_U + 64 * KIB, D, BF16)
    P.claim("RU")
    P.add("sp", lambda e: e.dma_start(out=gbc, in_=gain_row_d.to_broadcast([128, D])),
          reads=["RU"], writes=["gbc"], lane="gbc")
    for s in range(4):
        xs = cx.view(OFF_U + (s % 2) * 16 * KIB, D, F32)
        xtok = ("xs", s % 2)
        P.add("sp", lambda e, xs=xs, s=s: e.dma_start(out=xs, in_=src[r0 + 128 * s:r0 + 128 * (s + 1), :]),
              reads=["RU", ("dram", src_name, r0 // T)], writes=[xtok], lane=("xs", s % 2))
        ssq = stat[:, 16 + s:17 + s]
        rs = stat[:, 24 + s:25 + s]
        P.add("act", lambda e, s=s, ssq=ssq: e.activation(out=junk, in_=y[:, s, :], func=AF.Square, accum_out=ssq),
              reads=y_reads(s) + ["RU"], writes=["junk2", ("ssq2", s)])
        P.add("act", lambda e, ssq=ssq, rs=rs: e.activation(out=rs, in_=ssq, func=AF.Sqrt, scale=1.0 / D, bias=EPS),
              reads=[("ssq2", s)], writes=[("rs2", s)])
        P.add("dve", lambda e, rs=rs: e.reciprocal(out=rs, in_=rs), reads=[("rs2", s)], writes=[("rs2", s)])
        if res_scale != 1.0:
            P.add("dve", lambda e, rs=rs: e.tensor_scalar(out=rs, in0=rs, scalar1=float(res_scale), scalar2=None,
                                                         op0=ALU.mult),
                  reads=[("rs2", s)], writes=[("rs2", s)])
        P.add("pool", lambda e, s=s: e.tensor_tensor(out=tt, in0=y[:, s, :], in1=gbc, op=ALU.mult),
              reads=y_reads(s) + ["gbc", "RU"], writes=["tt"])
        P.add("dve", lambda e, xs=xs, rs=rs: e.scalar_tensor_tensor(out=xs, in0=tt, scalar=rs, in1=xs,
                                                                    op0=ALU.mult, op1=ALU.add),
              reads=["tt", ("rs2", s), "RU"], writes=[xtok])
        P.add("sp", lambda e, xs=xs, s=s: e.dma_start(out=dst[r0 + 128 * s:r0 + 128 * (s + 1), :], in_=xs),
              reads=[xtok, "RU"], writes=[("dram", dst_name, r0 // T)], lane=("st", dst_name),
              group=("st", dst_name, r0))


NH = 16
A_TMP = OFF_U
A_OUT = OFF_U + 20 * KIB
A_TM = OFF_U + 32 * KIB
A_MASK = OFF_S + 6 * KIB
A_LB = OFF_S + 8 * KIB
A_DEC = OFF_S + 9 * KIB


def phaseA_consts(cx, mask01_d, lbl_d):
    P = cx.P
    mk_ = cx.view(A_MASK, T, F32)
    lbv = cx.view(A_LB, 5 * 16, F32, "p (a b) -> p a b", b=16)
    P.add("sp", lambda e: e.dma_start(out=mk_, in_=mask01_d), writes=["mask01"], lane="c3")
    P.add("sp", lambda e: e.dma_start(out=lbv[:, 3:5, :], in_=lbl_d), writes=["lbl"], lane="c4")
    P.add("dve", lambda e: e.tensor_tensor(out=lbv[:, 0, :], in0=lbv[:, 3, :], in1=lbv[:, 4, :], op=ALU.subtract),
          reads=["lbl"], writes=["lb0"])
    P.add("act", lambda e: e.activation(out=lbv[:, 0, :], in_=lbv[:, 0, :], func=AF.Sigmoid), reads=["lb0"], writes=["lb"])
    P.add("dve", lambda e: e.tensor_scalar(out=lbv[:, 1, :], in0=lbv[:, 0, :], scalar1=-1.0, scalar2=1.0, op0=ALU.mult, op1=ALU.add),
          reads=["lb"], writes=["oml"])
    P.add("dve", lambda e: e.tensor_scalar(out=lbv[:, 2, :], in0=lbv[:, 0, :], scalar1=1.0, scalar2=-1.0, op0=ALU.mult, op1=ALU.add),
          reads=["lb"], writes=["noml"])


def phaseA_proj(cx, ring, w_in, fm_d, tm_d, dec_d, tile):
    P = cx.P
    hT = cx.view(OFF_H, KC * T, BF16, "p (a b) -> p a b", b=T)
    winr = w_in.rearrange("(kc p) f -> p kc f", p=128)
    mask01 = cx.view(A_MASK, T, F32)
    lbv = cx.view(A_LB, 5 * 16, F32, "p (a b) -> p a b", b=16)
    dec = cx.view(A_DEC, NH * 16, F32, "p (hf c) -> p hf c", c=8)
    tmp = [cx.view(A_TMP + i * 2 * KIB, T, F32) for i in range(9)]
    sig, qs, logf, kk, aa, dd, E1, E2, EA = tmp
    t0 = tile * T
    P.claim("RU")
    NQ = 2048
    for h in range(NH):
        cols = [0 * NQ + h * 128, 1 * NQ + h * 128, 3 * NQ + h * 128, 2 * NQ + h * 128,
                4 * NQ + h * 128, 5 * NQ + h * 128, 6 * NQ + h * 128]
        ofm = cx.view(A_OUT + (h % 2) * 6 * KIB, 6 * T, BF16, "p (a b) -> p a b", b=T)
        otm = cx.view(A_TM + (h % 2) * 2 * KIB, 4 * 2 * 128, BF16, "p (s f c) -> p s f c", s=4, f=2)
        ftok = ("ofm", h % 2)
        ttok = ("otm", h % 2)
        bankof = {0: 0, 1: 1, 2: 2, 3: 3, 4: 4, 5: 5, 6: 6}
        for pr in range(4):
            cc = cols[2 * pr:2 * pr + 2]
            parts = [(j * 8 * KIB, KC * 128, "p (a b) -> p a b", dict(b=128), winr[:, :, c:c + 128]) for j, c in enumerate(cc)]
            ri = ring.request(parts)
            if P.planning:
                continue
            vs = ring.views(ri)
            wtk = ring.tok(ri)
            for j, wvj in enumerate(vs):
                wi = 2 * pr + j
                bi = bankof[wi]
                if wi in (3, 6):
                    for s in range(4):
                        for kc in range(KC):
                            P.add("pe", lambda e, bi=bi, wvj=wvj, kc=kc, s=s: e.matmul(
                                cx.bank(bi)[:, s * 128:(s + 1) * 128], lhsT=hT[:, kc, s * 128:(s + 1) * 128], rhs=wvj[:, kc, :],
                                start=(kc == 0), stop=(kc == KC - 1)),
                                reads=[wtk, ("hT", kc, s), "RH"], writes=[("bank", bi)])
                else:
                    for kc in range(KC):
                        P.add("pe", lambda e, bi=bi, wvj=wvj, kc=kc: e.matmul(cx.bank(bi), lhsT=wvj[:, kc, :], rhs=hT[:, kc, :],
                                                                             start=(kc == 0), stop=(kc == KC - 1)),
                              reads=[wtk] + hT_reads(kc), writes=[("bank", bi)])
            ring.done(ri)
        if P.planning:
            continue
        lb = lbv[:, 0, h:h + 1]
        oml = lbv[:, 1, h:h + 1]
        noml = lbv[:, 2, h:h + 1]
        RU = ["RU"]
        P.add("act", lambda e: e.activation(out=sig, in_=cx.bank(1), func=AF.Sigmoid), reads=[("bank", 1)] + RU, writes=["sig"])
        P.add("act", lambda e, oml=oml, lb=lb: e.activation(out=logf, in_=sig, func=AF.Ln, scale=oml, bias=lb),
              reads=["sig", "oml", "lb"] + RU, writes=["logf"])
        P.add("dve", lambda e, oml=oml, noml=noml: e.tensor_scalar(out=kk, in0=sig, scalar1=noml, scalar2=oml, op0=ALU.mult, op1=ALU.add),
              reads=["sig", "oml", "noml"] + RU, writes=["kk"])
        P.add("dve", lambda e: e.tensor_tensor_scan(out=aa, data0=mask01, data1=logf, initial=0.0, op0=ALU.mult, op1=ALU.add),
              reads=["logf", "mask01"] + RU, writes=["aa"])
        a3 = aa.rearrange("p (c t) -> p c t", t=64)
        d3 = dd.rearrange("p (c t) -> p c t", t=64)
        P.add("dve", lambda e, a3=a3, d3=d3: e.tensor_tensor(out=d3, in0=a3, in1=a3[:, :, 31:32].to_broadcast([128, 8, 64]), op=ALU.subtract),
              reads=["aa"] + RU, writes=["dd"])
        P.add("act", lambda e: e.activation(out=E1, in_=dd, func=AF.Exp), reads=["dd"] + RU, writes=["E1"])
        P.add("act", lambda e: e.activation(out=E2, in_=dd, func=AF.Exp, scale=-1.0), reads=["dd"] + RU, writes=["E2"])
        P.add("act", lambda e: e.activation(out=EA, in_=aa, func=AF.Exp), reads=["aa"] + RU, writes=["EA"])
        P.add("act", lambda e: e.activation(out=qs, in_=cx.bank(0), func=AF.Silu), reads=[("bank", 0)] + RU, writes=["qs"])
        P.add("act", lambda e, ofm=ofm: e.activation(out=ofm[:, 3, :], in_=cx.bank(2), func=AF.Silu),
              reads=[("bank", 2)] + RU, writes=[ftok])
        P.add("dve", lambda e, ofm=ofm: e.tensor_tensor(out=ofm[:, 0, :], in0=qs, in1=E1, op=ALU.mult), reads=["qs", "E1"] + RU, writes=[ftok])
        P.add("dve", lambda e, ofm=ofm: e.tensor_tensor(out=ofm[:, 1, :], in0=kk, in1=E2, op=ALU.mult), reads=["kk", "E2"] + RU, writes=[ftok])
        P.add("dve", lambda e, ofm=ofm: e.tensor_tensor(out=ofm[:, 2, :], in0=qs, in1=EA, op=ALU.mult), reads=["qs", "EA"] + RU, writes=[ftok])
        E13 = E1.rearrange("p (c t) -> p c t", t=64)
        EA3 = EA.rearrange("p (c t) -> p c t", t=64)
        P.add("dve", lambda e, EA3=EA3, h=h: e.tensor_copy(out=dec[:, 2 * h, :], in_=EA3[:, :, 63]), reads=["EA"] + RU, writes=["dec"])
        P.add("dve", lambda e, E13=E13, h=h: e.tensor_copy(out=dec[:, 2 * h + 1, :], in_=E13[:, :, 63]), reads=["E1"] + RU, writes=["dec"])
        P.add("act", lambda e, ofm=ofm: e.activation(out=ofm[:, 4, :], in_=cx.bank(4), func=AF.Copy, scale=float(128 ** -0.5)),
              reads=[("bank", 4)] + RU, writes=[ftok])
        P.add("dve", lambda e, ofm=ofm: e.tensor_copy(out=ofm[:, 5, :], in_=cx.bank(5)), reads=[("bank", 5)] + RU, writes=[ftok])
        b3 = cx.bank(3).rearrange("p (s c) -> p s c", c=128)
        b6 = cx.bank(6).rearrange("p (s c) -> p s c", c=128)
        P.add("act", lambda e, otm=otm, b3=b3: e.activation(out=otm[:, :, 0, :], in_=b3, func=AF.Copy), reads=[("bank", 3)] + RU, writes=[ttok])
        P.add("dve", lambda e, otm=otm, b6=b6: e.tensor_copy(out=otm[:, :, 1, :], in_=b6), reads=[("bank", 6)] + RU, writes=[ttok])
        P.add("sp", lambda e, ofm=ofm, h=h: e.dma_start(out=fm_d[h].rearrange("f p t -> p f t")[:, :, t0:t0 + T], in_=ofm),
              reads=[ftok] + RU, lane=("stA", h % 2))
        for f in range(2):
            P.add("sp", lambda e, otm=otm, h=h, f=f: e.dma_start(
                out=tm_d[t0:t0 + T, h, f, :].rearrange("(s p) c -> p s c", p=128), in_=otm[:, :, f, :]),
                reads=[ttok] + RU, lane=("stA", h % 2))
    if P.planning:
        return
    P.add("sp", lambda e: e.dma_start(out=dec_d[:, :, tile * 8:(tile + 1) * 8], in_=dec), reads=["dec", "RU"], lane="stD")


SEQ = 8192
HL = 4
B_FM = 0
B_V = 32 * KIB
B_DEC = 48 * KIB
B_S = 52 * KIB
B_SCM = 57 * KIB
B_KTT = 58 * KIB
B_O = 60 * KIB
B_TRI = 70 * KIB
B_ONES = 71 * KIB
B_IDB = 71 * KIB + 256
B_NG = 71 * KIB + 512
C_KT = 72 * KIB
C_QT = 88 * KIB
C_V = 104 * KIB
C_BIAS = 120 * KIB
C_SC = 123 * KIB
C_PN = 128 * KIB
C_PT = 131 * KIB
C_OUT = 134 * KIB
C_ST = 136 * KIB
B_END = 137 * KIB


def mixer_consts(cx, ident_d, tri4_d, ones_d, ng_d, dec_d):
    P = cx.P
    P.add("pool", lambda e: e.dma_start(out=cx.view(B_IDB, 128, BF16), in_=ident_d), writes=["ident"], lane="c0")
    P.add("pool", lambda e: e.dma_start(out=cx.view(B_ONES, 128, BF16), in_=ones_d), writes=["ones"], lane="c0")
    P.add("sp", lambda e: e.dma_start(out=cx.view(B_TRI, HL * 64, F32)[0:64, :], in_=tri4_d), writes=["tri"], lane="c1")
    P.add("sp", lambda e: e.dma_start(out=cx.view(B_NG, HL, F32), in_=ng_d), writes=["ng"], lane="c1")
    P.add("sp", lambda e: e.dma_start(out=cx.view(B_DEC, HL * 2 * 128, F32, "p (h f c) -> p h f c", h=HL, f=2), in_=dec_d),
          writes=["decall"], lane="c1")


def hgrn_mixer(cx, fm_d, tm_d, oa_d):
    P = cx.P
    ident = cx.view(B_IDB, 128, BF16)
    ones = cx.view(B_ONES, 128, BF16)
    tri4 = cx.view(B_TRI, HL * 64, F32, "p (h t) -> p h t", h=HL)[0:64]
    ng = cx.view(B_NG, HL, F32)
    dec = cx.view(B_DEC, HL * 2 * 128, F32, "p (h f c) -> p h f c", h=HL, f=2)
    S = cx.view(B_S, HL * 128, F32, "p (h v) -> p h v", h=HL)
    tmpS = cx.view(B_S + 2 * KIB, HL * 128, F32, "p (h v) -> p h v", h=HL)
    Sbf = cx.view(B_S + 4 * KIB, HL * 128, BF16, "p (h v) -> p h v", h=HL)
    o_sb = cx.view(B_O, T, F32)
    sq = cx.view(B_O + 2 * KIB, T, BF16)
    rr = cx.view(B_O + 3 * KIB, T, F32)
    og = cx.view(B_O + 5 * KIB, T, F32)
    P.add("dve", lambda e: e.memset(S, 0.0), writes=["S"])
    P.add("dve", lambda e: e.memset(Sbf, 0.0), writes=["Sbf"])
    bSC, bKT, bSN, bSS = 4, 5, 6, 7
    ntile = SEQ // T
    for tile in range(ntile):
        t0 = tile * T
        par = tile % 2
        fm = cx.view(B_FM + par * 16 * KIB, HL * 4 * T, BF16, "p (h f t) -> p h f t", h=HL, f=4)
        vv = cx.view(B_V + par * 8 * KIB, 8 * HL * 128, BF16, "p (j h c) -> p j h c", j=8, h=HL)[0:64]
        ftok, vtok = ("fm", par), ("vv", par)
        for hl in range(HL):
            P.add("sp", lambda e, fm=fm, t0=t0, hl=hl: e.dma_start(
                out=fm[:, hl], in_=fm_d[hl, 0:4].rearrange("f p t -> p f t")[:, :, t0:t0 + T]),
                writes=[ftok], lane=("ldf", par), group=("ldf", tile))
            P.add("sp", lambda e, vv=vv, t0=t0, hl=hl: e.dma_start(
                out=vv[:, :, hl, :], in_=tm_d[t0:t0 + T, hl, 0, :].rearrange("(j p) c -> p j c", p=64)),
                writes=[vtok], lane=("ldv", par), group=("ldv", tile))
        for j in range(8):
            ch = tile * 8 + j
            cs = slice(j * 64, (j + 1) * 64)
            jp = j % 2
            scm = cx.view(B_SCM + jp * 512, HL * 64, BF16, "p (h t) -> p h t", h=HL)[0:64]
            ktT = cx.view(B_KTT + jp * KIB, HL * 128, BF16, "p (h k) -> p h k", h=HL)[0:64]
            for hl in range(HL):
                P.add("pe", lambda e, fm=fm, hl=hl, cs=cs: e.matmul(cx.bank(bSC)[0:64, hl * 64:(hl + 1) * 64],
                                                                   lhsT=fm[:, hl, 1, cs], rhs=fm[:, hl, 0, cs], start=True, stop=True),
                      reads=[ftok], writes=[("bank", bSC)])
            for hl in range(HL):
                P.add("pe", lambda e, fm=fm, hl=hl, cs=cs: e.transpose(out=cx.bank(bKT, BF16)[0:64, hl * 128:(hl + 1) * 128],
                                                                      in_=fm[:, hl, 1, cs], identity=ident),
                      reads=[ftok, "ident"], writes=[("bank", bKT)])
            P.add("dve", lambda e, scm=scm: e.tensor_tensor(
                out=scm, in0=cx.bank(bSC)[0:64, 0:HL * 64].rearrange("p (h t) -> p h t", h=HL), in1=tri4, op=ALU.mult),
                reads=[("bank", bSC), "tri"], writes=[("scm", jp)])
            P.add("act", lambda e, ktT=ktT: e.activation(
                out=ktT, in_=cx.bank(bKT, BF16)[0:64, 0:HL * 128].rearrange("p (h k) -> p h k", h=HL), func=AF.Copy),
                reads=[("bank", bKT)], writes=[("ktT", jp)])
            for hl in range(HL):
                P.add("pe", lambda e, vv=vv, scm=scm, hl=hl, j=j, cs=cs: e.matmul(
                    cx.bank(hl)[:, cs], lhsT=vv[:, j, hl, :], rhs=scm[:, hl, :], start=True, stop=False),
                    reads=[vtok, ("scm", jp)], writes=[("bank", hl)])
                P.add("pe", lambda e, fm=fm, hl=hl, cs=cs: e.matmul(
                    cx.bank(hl)[:, cs], lhsT=Sbf[:, hl, :], rhs=fm[:, hl, 2, cs], start=False, stop=True),
                    reads=[ftok, "Sbf"], writes=[("bank", hl)])
            for hl in range(HL):
                P.add("pe", lambda e, vv=vv, ktT=ktT, hl=hl, j=j: e.matmul(
                    cx.bank(bSN)[:, hl * 128:(hl + 1) * 128], lhsT=ktT[:, hl, :], rhs=vv[:, j, hl, :], start=True, stop=True),
                    reads=[vtok, ("ktT", jp)], writes=[("bank", bSN)])
            P.add("dve", lambda e, ch=ch: e.tensor_tensor(out=S, in0=S, in1=dec[:, :, 0, ch:ch + 1].to_broadcast([128, HL, 128]),
                                                         op=ALU.mult), reads=["S", "decall"], writes=["S"])
            P.add("dve", lambda e, ch=ch: e.tensor_tensor(
                out=tmpS, in0=cx.bank(bSN).rearrange("p (h v) -> p h v", h=HL),
                in1=dec[:, :, 1, ch:ch + 1].to_broadcast([128, HL, 128]), op=ALU.mult),
                reads=[("bank", bSN), "decall"], writes=["tmpS"])
            P.add("dve", lambda e: e.tensor_tensor(out=S, in0=S, in1=tmpS, op=ALU.add), reads=["S", "tmpS"], writes=["S"])
            P.add("act", lambda e: e.activation(out=Sbf, in_=S, func=AF.Copy), reads=["S"], writes=["Sbf"])
        for hl in range(HL):
            outb = cx.view(B_O + 7 * KIB + (hl % 2) * KIB, T, BF16)
            otok = ("outb", hl % 2)
            P.add("dve", lambda e, hl=hl: e.tensor_copy(out=o_sb, in_=cx.bank(hl)), reads=[("bank", hl)], writes=["o_sb"])
            P.add("act", lambda e: e.activation(out=sq, in_=o_sb, func=AF.Square), reads=["o_sb"], writes=["sq"])
            P.add("pe", lambda e: e.matmul(cx.bank(bSS), lhsT=ones, rhs=sq, start=True, stop=True),
                  reads=["ones", "sq"], writes=[("bank", bSS)])
            P.add("act", lambda e: e.activation(out=rr, in_=cx.bank(bSS), func=AF.Sqrt, scale=1.0 / 128, bias=EPS),
                  reads=[("bank", bSS)], writes=["rr"])
            P.add("dve", lambda e: e.reciprocal(out=rr, in_=rr), reads=["rr"], writes=["rr"])
            P.add("dve", lambda e: e.tensor_tensor(out=og, in0=o_sb, in1=rr, op=ALU.mult), reads=["o_sb", "rr"], writes=["og"])
            P.add("dve", lambda e, outb=outb, fm=fm, hl=hl: e.scalar_tensor_tensor(
                out=outb, in0=og, scalar=ng[:, hl:hl + 1], in1=fm[:, hl, 3, :], op0=ALU.mult, op1=ALU.mult),
                reads=["og", "ng", ftok], writes=[otok])
            P.add("sp", lambda e, outb=outb, hl=hl, t0=t0: e.dma_start(out=oa_d[hl, :, t0:t0 + T], in_=outb),
                  reads=[otok], lane=("stO", hl % 2))


def attn_mixer(cx, fm_d, tm_d, bias_d, ob_d):
    P = cx.P
    ident = cx.view(B_IDB, 128, BF16)
    KT = cx.view(C_KT, SEQ, BF16)
    QT = cx.view(C_QT, SEQ, BF16)
    V = cx.view(C_V, 64 * 128, BF16, "p (s c) -> p s c", c=128)
    bias = cx.view(C_BIAS, 640, F32)
    stat = cx.view(C_ST, 16, F32)
    NB = SEQ // 128
    for hl in range(HL):
        P.add("sp", lambda e, hl=hl: e.dma_start(out=KT, in_=fm_d[hl, 5]), writes=["KT"], lane="ldk")
        P.add("sp", lambda e, hl=hl: e.dma_start(out=QT, in_=fm_d[hl, 4]), writes=["QT"], lane="ldq")
        for q4 in range(4):
            P.add("sp", lambda e, hl=hl, q4=q4: e.dma_start(
                out=V[:, q4 * 16:(q4 + 1) * 16, :],
                in_=tm_d[q4 * 2048:(q4 + 1) * 2048, hl, 1, :].rearrange("(s p) c -> p s c", p=128)),
                writes=[("V", q4)], lane="ldvv")
        P.add("sp", lambda e, hl=hl: e.dma_start(out=bias, in_=bias_d[hl]), writes=["bias"], lane="ldb")
        P.add("pool", lambda e: e.memset(bias[0:64, 576:640], -30000.0), writes=["bias"])
        P.add("pool", lambda e: e.memset(bias[64:128, 0:64], -30000.0), writes=["bias"])
        for qb in range(NB):
            par = qb % 2
            b1, b2, b3, b4 = (0, 1, 2, 3) if par == 0 else (4, 5, 6, 7)
            j0 = max(0, 4 - qb) * 128
            kbs = list(range(j0 // 128, 5))
            kpos0 = qb * 128 - 512
            sc = cx.view(C_SC + par * 2560, 640, F32)
            pn = cx.view(C_PN + par * 1280, 640, BF16)
            pT = cx.view(C_PT + par * 1280, 640, BF16)
            outbuf = cx.view(C_OUT + ((qb // 4) % 2) * KIB, T, BF16)
            otok = ("aout", (qb // 4) % 2)
            nmx = stat[:, par * 4:par * 4 + 1]
            rsum = stat[:, par * 4 + 1:par * 4 + 2]
            qsl = slice(qb * 128, (qb + 1) * 128)
            vreads = [("V", q) for q in sorted(set((max(0, qb - 4) * 128) // 2048 for _ in [0]) | {(qb * 128) // 2048})]
            if j0 < 512:
                P.add("pe", lambda e, b1=b1, j0=j0, qsl=qsl, kpos0=kpos0: e.matmul(
                    cx.bank(b1)[:, j0:512], lhsT=QT[:, qsl], rhs=KT[:, kpos0 + j0:kpos0 + 512], start=True, stop=True),
                    reads=["QT", "KT"], writes=[("bank", b1)])
            P.add("pe", lambda e, b2=b2, qsl=qsl, kpos0=kpos0: e.matmul(
                cx.bank(b2)[:, 0:128], lhsT=QT[:, qsl], rhs=KT[:, kpos0 + 512:kpos0 + 640], start=True, stop=True),
                reads=["QT", "KT"], writes=[("bank", b2)])
            if j0 < 512:
                P.add("dve", lambda e, sc=sc, b1=b1, j0=j0: e.tensor_tensor(out=sc[:, j0:512], in0=cx.bank(b1)[:, j0:512],
                                                                           in1=bias[:, j0:512], op=ALU.add),
                      reads=[("bank", b1), "bias"], writes=[("sc", par)])
            P.add("dve", lambda e, sc=sc, b2=b2: e.tensor_tensor(out=sc[:, 512:640], in0=cx.bank(b2)[:, 0:128],
                                                                in1=bias[:, 512:640], op=ALU.add),
                  reads=[("bank", b2), "bias"], writes=[("sc", par)])
            P.add("dve", lambda e, sc=sc, nmx=nmx, j0=j0: e.tensor_reduce(out=nmx, in_=sc[:, j0:640], axis=AX.X, op=ALU.max, negate=True),
                  reads=[("sc", par)], writes=[("nmx", par)])
            P.add("act", lambda e, sc=sc, nmx=nmx, rsum=rsum, j0=j0: e.activation(
                out=sc[:, j0:640], in_=sc[:, j0:640], func=AF.Exp, bias=nmx, accum_out=rsum),
                reads=[("nmx", par)], writes=[("sc", par), ("rsum", par)])
            P.add("dve", lambda e, rsum=rsum: e.reciprocal(out=rsum, in_=rsum), reads=[("rsum", par)], writes=[("rsum", par)])
            P.add("dve", lambda e, sc=sc, pn=pn, rsum=rsum, j0=j0: e.tensor_scalar(
                out=pn[:, j0:640], in0=sc[:, j0:640], scalar1=rsum, scalar2=None, op0=ALU.mult),
                reads=[("sc", par), ("rsum", par)], writes=[("pn", par)])
            for kb in kbs:
                P.add("pe", lambda e, b3=b3, pn=pn, kb=kb: e.transpose(
                    out=cx.bank(b3, BF16)[:, kb * 128:(kb + 1) * 128], in_=pn[:, kb * 128:(kb + 1) * 128], identity=ident),
                    reads=[("pn", par), "ident"], writes=[("bank", b3)])
            P.add("act", lambda e, pT=pT, b3=b3, j0=j0: e.activation(out=pT[:, j0:640], in_=cx.bank(b3, BF16)[:, j0:640], func=AF.Copy),
                  reads=[("bank", b3)], writes=[("pT", par)])
            for i, kb in enumerate(kbs):
                ksub = qb - 4 + kb
                P.add("pe", lambda e, b4=b4, pT=pT, kb=kb, ksub=ksub, i=i, n=len(kbs): e.matmul(
                    cx.bank(b4)[:, 0:128], lhsT=V[:, ksub, :], rhs=pT[:, kb * 128:(kb + 1) * 128],
                    start=(i == 0), stop=(i == n - 1)),
                    reads=[("pT", par), ("V", ksub // 16)], writes=[("bank", b4)])
            oc = (qb % 4) * 128
            P.add("dve", lambda e, outbuf=outbuf, b4=b4, oc=oc: e.tensor_copy(out=outbuf[:, oc:oc + 128], in_=cx.bank(b4)[:, 0:128]),
                  reads=[("bank", b4)], writes=[otok])
            if qb % 4 == 3:
                tt0 = (qb // 4) * T
                P.add("sp", lambda e, outbuf=outbuf, hl=hl, tt0=tt0: e.dma_start(out=ob_d[hl, :, tt0:tt0 + T], in_=outbuf),
                      reads=[otok], lane=("stB", (qb // 4) % 2))


C_M = OFF_U
C_OA = OFF_U + 32 * KIB
C_OB = OFF_U + 48 * KIB
C_T1 = OFF_U + 64 * KIB
S_BG = OFF_S + 10 * KIB + 512


def phaseC_consts(cx, bg_d):
    cx.P.add("sp", lambda e: e.dma_start(out=cx.view(S_BG, 2 * KC, F32), in_=bg_d), writes=["bg"], lane="c5")


def phaseC_mix(cx, ring, w_in, w_up_a, w_up_b, w_out, oa_d, ob_d, tile):
    P = cx.P
    hT = cx.view(OFF_H, KC * T, BF16, "p (a b) -> p a b", b=T)
    y = cx.view(OFF_H, 4 * D, BF16, "p (a b) -> p a b", b=D)
    mT = cx.view(C_M, KC * T, BF16, "p (a b) -> p a b", b=T)
    oaT = cx.view(C_OA, 16 * T, BF16, "p (a b) -> p a b", b=T)
    obT = cx.view(C_OB, 16 * T, BF16, "p (a b) -> p a b", b=T)
    sA, sB, m1, m2 = [cx.view(C_T1 + i * 2 * KIB, T, F32) for i in range(4)]
    bg = cx.view(S_BG, 2 * KC, F32, "p (a b) -> p a b", b=KC)
    winr = w_in.rearrange("(kc p) f -> p kc f", p=128)
    upa = w_up_a.rearrange("(kc p) f -> p kc f", p=128)
    upb = w_up_b.rearrange("(kc p) f -> p kc f", p=128)
    woutr = w_out.rearrange("(kc p) f -> p kc f", p=128)
    t0 = tile * T
    GA0 = 0
    GB0 = D
    if not P.planning:
        P.claim("RU")
        P.add("sp", lambda e: e.dma_start(out=oaT, in_=oa_d.rearrange("h p t -> p h t")[:, :, t0:t0 + T]),
              reads=["RU"], writes=["oaT"], lane="ldoa")
        P.add("sp", lambda e: e.dma_start(out=obT, in_=ob_d.rearrange("h p t -> p h t")[:, :, t0:t0 + T]),
              reads=["RU"], writes=["obT"], lane="ldob")
    for c in range(KC):
        p1 = [(0, KC * 128, "p (a b) -> p a b", dict(b=128), winr[:, :, GA0 + c * 128:GA0 + (c + 1) * 128]),
              (8 * KIB, KC * 128, "p (a b) -> p a b", dict(b=128), winr[:, :, GB0 + c * 128:GB0 + (c + 1) * 128])]
        p2 = [(0, 16 * 128, "p (a b) -> p a b", dict(b=128), upa[:, :, c * 128:(c + 1) * 128]),
              (4 * KIB, 16 * 128, "p (a b) -> p a b", dict(b=128), upb[:, :, c * 128:(c + 1) * 128])]
        pb = (c % 2) * 4
        bGA, bGB, bYA, bYB = pb, pb + 1, pb + 2, pb + 3
        r1 = ring.request(p1)
        if not P.planning:
            wga, wgb = ring.views(r1)
            t1 = ring.tok(r1)
            for (bk, w, tk) in ((bGA, wga, t1), (bGB, wgb, t1)):
                for kc in range(KC):
                    P.add("pe", lambda e, bk=bk, w=w, kc=kc: e.matmul(cx.bank(bk), lhsT=w[:, kc, :], rhs=hT[:, kc, :],
                                                                     start=(kc == 0), stop=(kc == KC - 1)),
                          reads=[tk] + hT_reads(kc), writes=[("bank", bk)])
            ring.done(r1)
        r2 = ring.request(p2)
        if P.planning:
            continue
        wua, wub = ring.views(r2)
        t2 = ring.tok(r2)
        for (bk, w, src, stok) in ((bYA, wua, oaT, "oaT"), (bYB, wub, obT, "obT")):
            for kc in range(16):
                P.add("pe", lambda e, bk=bk, w=w, kc=kc, src=src: e.matmul(cx.bank(bk), lhsT=w[:, kc, :], rhs=src[:, kc, :],
                                                                          start=(kc == 0), stop=(kc == 15)),
                      reads=[t2, stok, "RU"], writes=[("bank", bk)])
        ring.done(r2)
        P.add("act", lambda e, bGA=bGA, c=c: e.activation(out=sA, in_=cx.bank(bGA), func=AF.Sigmoid, bias=bg[:, 0, c:c + 1]),
              reads=[("bank", bGA), "bg", "RU"], writes=["sA"])
        P.add("act", lambda e, bGB=bGB, c=c: e.activation(out=sB, in_=cx.bank(bGB), func=AF.Sigmoid, bias=bg[:, 1, c:c + 1]),
              reads=[("bank", bGB), "bg", "RU"], writes=["sB"])
        P.add("dve", lambda e, bYA=bYA: e.tensor_tensor(out=m1, in0=sA, in1=cx.bank(bYA), op=ALU.mult),
              reads=["sA", ("bank", bYA), "RU"], writes=["m1"])
        P.add("dve", lambda e, bYB=bYB: e.tensor_tensor(out=m2, in0=sB, in1=cx.bank(bYB), op=ALU.mult),
              reads=["sB", ("bank", bYB), "RU"], writes=["m2"])
        P.add("pool", lambda e, c=c: e.tensor_tensor(out=mT[:, c, :], in0=m1, in1=m2, op=ALU.add),
              reads=["m1", "m2", "RU"], writes=[("mT", c)])
    if not P.planning:
        P.claim("RH")
    for cb in range(16):
        parts = [(0, KC * 256, "p (a b) -> p a b", dict(b=256), woutr[:, :, cb * 256:(cb + 1) * 256])]
        ri = ring.request(parts)
        if P.planning:
            continue
        (wv,) = ring.views(ri)
        wtok = ring.tok(ri)
        for s in range(4):
            bi = (cb * 4 + s) % 8
            for kc in range(KC):
                P.add("pe", lambda e, bi=bi, kc=kc, s=s, wv=wv: e.matmul(
                    cx.bank(bi)[:, 0:256], lhsT=mT[:, kc, s * 128:(s + 1) * 128], rhs=wv[:, kc, :],
                    start=(kc == 0), stop=(kc == KC - 1)),
                    reads=[wtok, ("mT", kc), "RU"], writes=[("bank", bi)])
            if s == 3:
                ring.done(ri)
            dst = y[:, s, cb * 256:(cb + 1) * 256]
            if bi % 2 == 0:
                P.add("act", lambda e, dst=dst, bi=bi: e.activation(out=dst, in_=cx.bank(bi)[:, 0:256], func=AF.Copy),
                      reads=[("bank", bi), "RH"], writes=[("y", s, cb // 4, (cb // 2) % 2)])
            else:
                P.add("dve", lambda e, dst=dst, bi=bi: e.tensor_copy(out=dst, in_=cx.bank(bi)[:, 0:256]),
                      reads=[("bank", bi), "RH"], writes=[("y", s, cb // 4, (cb // 2) % 2)])


from concourse.bass_utils import run_bass_kernel_spmd

NFC_FULL = 86
DFF = NFC_FULL * 128
NTOK = 2048
NT = NTOK // T


def _run_body(cx, body):
    cx.P.planning = True
    body()
    cx.P.planning = False
    body()


def build_A():
    nc = bass.Bass("TRN2", target_bir_lowering=False)
    dt = lambda n, s, d=F32, k="ExternalInput": nc.dram_tensor(n, s, d, kind=k).ap()
    x = dt("x", [NTOK, D])
    w1 = dt("w1", [D, DFF]); w3 = dt("w3", [D, DFF]); w2 = dt("w2", [DFF, D])
    w_in = dt("w_in", [D, 7 * 2048])
    gfm = dt("gfm", [128, 3 * KC]); gpost = dt("gpost", [1, D])
    ident = dt("ident", [128, 128]); mask01 = dt("mask01", [128, T]); lbl = dt("lbl", [128, 2, 16])
    x1 = dt("x1", [NTOK, D], F32, "ExternalOutput")
    fm = dt("fm", [NH, 6, 128, NTOK], BF16, "ExternalOutput")
    tm = dt("tm", [NTOK, NH, 2, 128], BF16, "ExternalOutput")
    dec = dt("dec", [128, NH * 2, NTOK // 64], F32, "ExternalOutput")
    with ExitStack() as st:
        cx = Ctx(nc, st, arena_kib=180)
        ring = WRing(cx, OFF_W, WSLOT, NSLOT)

        def body():
            if not cx.P.planning:
                emit_consts(cx, ident, gfm)
                phaseA_consts(cx, mask01, lbl)
            for t in range(NT):
                if not cx.P.planning:
                    prologue(cx, x, "x", t * T, 0)
                ffn_core(cx, ring, w1, w3, w2, NFC_FULL)
                if not cx.P.planning:
                    epilogue(cx, x, "x", x1, "x1", t * T, gpost, 0.5)
                    prologue(cx, x1, "x1", t * T, 1)
                phaseA_proj(cx, ring, w_in, fm, tm, dec, t)
            if not cx.P.planning:
                cx.P.final_wait()

        _run_body(cx, body)
        cx.P.emit(nc, st)
    return nc


def build_B():
    nc = bass.Bass("TRN2", target_bir_lowering=False)
    dt = lambda n, s, d=F32, k="ExternalInput": nc.dram_tensor(n, s, d, kind=k).ap()
    fm = dt("fm2", [HL, 6, 128, SEQ], BF16)
    tm = dt("tm2", [SEQ, HL, 2, 128], BF16)
    dec = dt("dec2", [128, HL, 2, SEQ // 64])
    bias = dt("bias2", [HL, 128, 640])
    ng = dt("ng2", [128, HL])
    ident = dt("ident", [128, 128]); tri4 = dt("tri4", [64, HL * 64]); ones = dt("ones", [128, 128])
    oa = dt("oa", [HL, 128, SEQ], BF16, "ExternalOutput")
    ob = dt("ob", [HL, 128, SEQ], BF16, "ExternalOutput")
    with ExitStack() as st:
        cx = Ctx(nc, st, arena_kib=140)

        def body():
            mixer_consts(cx, ident, tri4, ones, ng, dec)
            hgrn_mixer(cx, fm, tm, oa)
            attn_mixer(cx, fm, tm, bias, ob)
            cx.P.final_wait()

        body()
        cx.P.emit(nc, st)
    return nc


def build_C():
    nc = bass.Bass("TRN2", target_bir_lowering=False)
    dt = lambda n, s, d=F32, k="ExternalInput": nc.dram_tensor(n, s, d, kind=k).ap()
    x1 = dt("x1", [NTOK, D])
    oa = dt("oa3", [NH, 128, NTOK], BF16); ob = dt("ob3", [NH, 128, NTOK], BF16)
    w_g = dt("w_g", [D, 2 * D]); w_up_a = dt("w_up_a", [2048, D]); w_up_b = dt("w_up_b", [2048, D]); w_out = dt("w_out", [D, D])
    w1 = dt("w1", [D, DFF]); w3 = dt("w3", [D, DFF]); w2 = dt("w2", [DFF, D])
    gfm = dt("gfm", [128, 3 * KC]); gpost_m = dt("gpost_m", [1, D]); gpost_2 = dt("gpost_2", [1, D])
    bgd = dt("bg", [128, 2 * KC]); ident = dt("ident", [128, 128])
    x2 = dt("x2", [NTOK, D], F32, "ExternalOutput")
    out = dt("out", [NTOK, D], F32, "ExternalOutput")
    with ExitStack() as st:
        cx = Ctx(nc, st, arena_kib=180)
        ring = WRing(cx, OFF_W, WSLOT, NSLOT)

        def body():
            if not cx.P.planning:
                emit_consts(cx, ident, gfm)
                phaseC_consts(cx, bgd)
            for t in range(NT):
                if not cx.P.planning:
                    prologue(cx, x1, "x1", t * T, 0)
                phaseC_mix(cx, ring, w_g, w_up_a, w_up_b, w_out, oa, ob, t)
                if not cx.P.planning:
                    epilogue(cx, x1, "x1", x2, "x2", t * T, gpost_m, 1.0)
                    prologue(cx, x2, "x2", t * T, 1)
                ffn_core(cx, ring, w1, w3, w2, NFC_FULL)
                if not cx.P.planning:
                    epilogue(cx, x2, "x2", out, "out", t * T, gpost_2, 0.5)
            if not cx.P.planning:
                cx.P.final_wait()

        _run_body(cx, body)
        cx.P.emit(nc, st)
    return nc


def _fm_gain(*gs):
    g = np.zeros((128, 3 * KC), np.float32)
    for i, v in enumerate(gs):
        g[:, i * KC:(i + 1) * KC] = np.asarray(v, np.float32).reshape(KC, 128).T
    return g


def kernel(**inp):
    import ml_dtypes
    f32 = np.float32
    g = {k: np.asarray(v) for k, v in inp.items()}
    NCORE = 8
    cores = list(range(NCORE))
    xs = np.ascontiguousarray(g["x"], dtype=f32).reshape(NCORE, NTOK, D)
    ident = np.eye(128, dtype=f32)
    mask01 = np.ones((128, T), f32)
    mask01[:, ::64] = 0.0
    lbl = np.ascontiguousarray(np.transpose(g["hgrn_lb_logits"].astype(f32).reshape(2, NH, 128), (2, 0, 1)))
    w_in = g["w_in"][0]
    ncA = build_A()
    commonA = dict(w1=g["ffn1_w1"][0], w3=g["ffn1_w3"][0], w2=g["ffn1_w2"][0],
                   w_in=np.ascontiguousarray(w_in[:, :7 * 2048]),
                   gfm=_fm_gain(g["ffn1_pre_g"][0], g["mix_pre_g"][0]), gpost=g["ffn1_post_g"].astype(f32).reshape(1, D),
                   ident=ident, mask01=mask01, lbl=lbl)
    resA = run_bass_kernel_spmd(ncA, [dict(commonA, x=xs[r]) for r in cores], core_ids=cores).results
    idx = np.clip(np.arange(128)[:, None] + 512 - np.arange(640)[None, :], -256, 256) + 256
    rb = g["rel_bias"][0].astype(f32)
    ngf = g["hgrn_norm_g"][0].astype(f32).reshape(NH, 128)
    tri4 = np.tile(np.triu(np.ones((64, 64), f32))[:, None, :], (1, HL, 1)).reshape(64, HL * 64)
    ones = np.ones((128, 128), f32)
    inB = []
    for r in cores:
        b, hg = r // 4, r % 4
        hs = slice(hg * HL, (hg + 1) * HL)
        src = [resA[b * 4 + q] for q in range(4)]
        inB.append(dict(
            fm2=np.ascontiguousarray(np.concatenate([s["fm"][hs] for s in src], axis=3)),
            tm2=np.ascontiguousarray(np.concatenate([s["tm"][:, hs] for s in src], axis=0)),
            dec2=np.ascontiguousarray(np.concatenate([s["dec"].reshape(128, NH, 2, NTOK // 64)[:, hs] for s in src], axis=3)),
            bias2=np.ascontiguousarray(rb[hs][:, idx]),
            ng2=np.ascontiguousarray(ngf[hs].T), ident=ident, tri4=tri4, ones=ones))
    ncB = build_B()
    resB = run_bass_kernel_spmd(ncB, inB, core_ids=cores).results
    ncC = build_C()
    bg = np.ascontiguousarray(np.transpose(g["b_gate"][0].astype(f32).reshape(2, KC, 128), (2, 0, 1))).reshape(128, 2 * KC)
    commonC = dict(w_g=np.ascontiguousarray(w_in[:, 7 * 2048:]), w_up_a=g["w_up_a"][0], w_up_b=g["w_up_b"][0],
                   w_out=g["w_out"][0], w1=g["ffn2_w1"][0], w3=g["ffn2_w3"][0], w2=g["ffn2_w2"][0],
                   gfm=_fm_gain(g["mix_pre_g"][0], g["ffn2_pre_g"][0]),
                   gpost_m=g["mix_post_g"].astype(f32).reshape(1, D), gpost_2=g["ffn2_post_g"].astype(f32).reshape(1, D),
                   bg=bg, ident=ident)
    inC = []
    for r in cores:
        b, q = r // 4, r % 4
        ts = slice(q * NTOK, (q + 1) * NTOK)
        inC.append(dict(commonC, x1=resA[r]["x1"],
                        oa3=np.ascontiguousarray(np.concatenate([resB[b * 4 + hg]["oa"][:, :, ts] for hg in range(4)], axis=0)),
                        ob3=np.ascontiguousarray(np.concatenate([resB[b * 4 + hg]["ob"][:, :, ts] for hg in range(4)], axis=0))))
    resC = run_bass_kernel_spmd(ncC, inC, core_ids=cores).results
    out = np.concatenate([resC[r]["out"] for r in cores], axis=0).reshape(2, SEQ, D).astype(f32)
    return out
```
